# Optimizing a Trainium2 kernel written in Bass

```python
import math
import jax, jax.numpy as jnp
from jax import lax
import numpy as np

D_MODEL = 2048
BATCH = 4
SEQ = 4096
DEPTH = 2

N_META = 16
BLOCK = 128
PAD = BLOCK - N_META

D_MIX = D_MODEL
ATTN_HEADS = 8
ATTN_HEAD_DIM = 64
D_ATTN = ATTN_HEADS * ATTN_HEAD_DIM
POOL_WINDOWS = (2, 4, 8, 16)
POOL_GROUPS = 4
D_POOL = 512
POOL_GROUP_DIM = D_POOL // POOL_GROUPS
D_SSD = D_MIX - D_ATTN - D_POOL
SSD_HEAD_DIM = 64
SSD_HEADS = D_SSD // SSD_HEAD_DIM
SSD_GROUPS = 2
SSD_HEADS_PER_GROUP = SSD_HEADS // SSD_GROUPS
SSD_STATE = 128
CONV_K = 4
D_CONV = D_SSD + 2 * SSD_GROUPS * SSD_STATE
SPLIT_SIZES = (D_ATTN, D_ATTN, D_ATTN, ATTN_HEADS, D_POOL, D_SSD, D_CONV, SSD_HEADS)
D_IN = sum(SPLIT_SIZES)
D_FF = 5632
ALPHA = (2 * DEPTH) ** 0.25
BETA = (8 * DEPTH) ** -0.25
LN_EPS = 1e-5
RMS_EPS = 1e-5
NEG_INF = -1e30

kernel_name = "hybrid_fox_pool_ssd_macaron_deepnorm"


def layer_norm(x, g, b):
    xf = x.astype(jnp.float32)
    mu = jnp.mean(xf, axis=-1, keepdims=True)
    var = jnp.mean(jnp.square(xf - mu), axis=-1, keepdims=True)
    return ((xf - mu) * lax.rsqrt(var + LN_EPS) * g + b).astype(x.dtype)


def swiglu(x, w_gate, w_up, w_down):
    return (jax.nn.silu(x @ w_gate) * (x @ w_up)) @ w_down


def pad_front(a):
    return jnp.pad(a, [(0, 0), (PAD, 0)] + [(0, 0)] * (a.ndim - 2))


def forgetting_attention(q, k, v, f_logit):
    b, l = q.shape[:2]
    lp = l + PAD
    nb = lp // BLOCK
    log_f = jax.nn.log_sigmoid(f_logit.astype(jnp.float32))
    c = jnp.cumsum(pad_front(log_f), axis=1).transpose(0, 2, 1)
    kf = pad_front(k).astype(jnp.float32)
    vf = pad_front(v).astype(jnp.float32)
    q_blocks = pad_front(q).astype(jnp.float32).reshape(
        b, nb, BLOCK, ATTN_HEADS, ATTN_HEAD_DIM).transpose(1, 0, 2, 3, 4)
    c_blocks = c.reshape(b, ATTN_HEADS, nb, BLOCK).transpose(2, 0, 1, 3)
    key_pos = jnp.arange(lp)
    scale = ATTN_HEAD_DIM ** -0.5

    def one_block(args):
        qb, cb, start = args
        s = jnp.einsum('bqhd,bkhd->bhqk', qb, kf) * scale
        s = s + cb[..., :, None] - c[:, :, None, :]
        q_pos = start + jnp.arange(BLOCK)
        mask = (key_pos[None, :] <= q_pos[:, None]) & (key_pos[None, :] >= PAD)
        p = jax.nn.softmax(jnp.where(mask, s, NEG_INF), axis=-1)
        return jnp.einsum('bhqk,bkhd->bqhd', p, vf)

    out = lax.map(one_block, (q_blocks, c_blocks, jnp.arange(nb) * BLOCK))
    out = out.transpose(1, 0, 2, 3, 4).reshape(b, lp, D_ATTN)[:, PAD:]
    return out.astype(q.dtype)


def multiscale_pool(u, pool_w, pool_scale):
    b, l, _ = u.shape
    uf = u.astype(jnp.float32).reshape(b, l, POOL_GROUPS, POOL_GROUP_DIM)
    cs0 = jnp.concatenate([jnp.zeros((b, 1, POOL_GROUPS, POOL_GROUP_DIM), jnp.float32),
                           jnp.cumsum(uf, axis=1)], axis=1)
    t = jnp.arange(l)
    means = []
    for g, w in enumerate(POOL_WINDOWS):
        upper = cs0[:, 1:, g]
        lower = jnp.concatenate([jnp.zeros((b, w - 1, POOL_GROUP_DIM), jnp.float32),
                                 cs0[:, :l + 1 - w, g]], axis=1)
        cnt = jnp.minimum(t + 1, w).astype(jnp.float32)
        means.append((upper - lower) / cnt[None, :, None])
    pooled = jnp.stack(means, axis=2)
    mixed = jnp.einsum('blgc,gcd->blgd', pooled - uf, pool_w)
    out = mixed.reshape(b, l, D_POOL) * pool_scale
    return out.astype(u.dtype)


def ssd_mixer(z, xbc, dt_raw, conv_w, conv_b, dt_bias, a_log, d_skip, norm_w):
    b, l, _ = z.shape
    lp = l + PAD
    nc = lp // BLOCK
    G, R, P, N = SSD_GROUPS, SSD_HEADS_PER_GROUP, SSD_HEAD_DIM, SSD_STATE
    xbc = lax.conv_general_dilated(xbc, conv_w[:, None, :], window_strides=(1,),
                                   padding=[(CONV_K - 1, 0)],
                                   dimension_numbers=('NWC', 'WIO', 'NWC'),
                                   feature_group_count=D_CONV) + conv_b
    xbc = jax.nn.silu(xbc.astype(jnp.float32))
    xs, bm, cm = jnp.split(xbc, [D_SSD, D_SSD + G * N], axis=-1)
    dt = jax.nn.softplus(dt_raw.astype(jnp.float32) + dt_bias)
    a = -jnp.exp(a_log.astype(jnp.float32)).reshape(G, R)
    xs = pad_front(xs).reshape(b, nc, BLOCK, G, R, P)
    bm = pad_front(bm).reshape(b, nc, BLOCK, G, N)
    cm = pad_front(cm).reshape(b, nc, BLOCK, G, N)
    dt_p = pad_front(dt).reshape(b, nc, BLOCK, G, R)
    x_dt = xs * dt_p[..., None]
    a_blk = (dt_p * a).transpose(0, 3, 4, 1, 2)
    a_cs = jnp.cumsum(a_blk, axis=-1)
    seg = a_cs[..., :, None] - a_cs[..., None, :]
    causal = jnp.tril(jnp.ones((BLOCK, BLOCK), dtype=bool))
    decay = jnp.exp(jnp.where(causal, seg, -jnp.inf))
    cb = jnp.einsum('bclgn,bcsgn->bgcls', cm, bm)
    y_diag = jnp.einsum('bgcls,bgrcls,bcsgrp->bclgrp', cb, decay, x_dt)
    decay_states = jnp.exp(a_cs[..., -1:] - a_cs)
    states = jnp.einsum('bclgn,bgrcl,bclgrp->bcgrpn', bm, decay_states, x_dt)
    chunk_decay = jnp.exp(a_cs[..., -1])

    def step(h, inp):
        s, d = inp
        return d[..., None, None] * h + s, h

    h0 = jnp.zeros((b, G, R, P, N), jnp.float32)
    _, prev = lax.scan(step, h0, (states.transpose(1, 0, 2, 3, 4, 5),
                                  chunk_decay.transpose(3, 0, 1, 2)))
    prev = prev.transpose(1, 0, 2, 3, 4, 5)
    y_off = jnp.einsum('bclgn,bcgrpn,bgrcl->bclgrp', cm, prev, jnp.exp(a_cs))
    y = y_diag + y_off + xs * d_skip.reshape(G, R)[..., None]
    y = y.reshape(b, lp, D_SSD)[:, PAD:]
    gy = (y * jax.nn.silu(z.astype(jnp.float32))).reshape(b, l, G, D_SSD // G)
    gy = gy * lax.rsqrt(jnp.mean(jnp.square(gy), axis=-1, keepdims=True) + RMS_EPS)
    return (gy.reshape(b, l, D_SSD) * norm_w).astype(z.dtype)


def hybrid_mixer(h, w_in, b_fgate, pool_w, pool_scale, conv_w, conv_b, dt_bias, a_log,
                 d_skip, ssd_norm_w, w_out):
    b, l, _ = h.shape
    proj = h @ w_in
    cuts = [int(v) for v in np.cumsum(SPLIT_SIZES)[:-1]]
    q, k, v, f_logit, u, z, xbc, dt_raw = jnp.split(proj, cuts, axis=-1)
    heads = (b, l, ATTN_HEADS, ATTN_HEAD_DIM)
    y_a = forgetting_attention(q.reshape(heads), k.reshape(heads), v.reshape(heads),
                               f_logit + b_fgate)
    y_b = multiscale_pool(u, pool_w, pool_scale)
    y_c = ssd_mixer(z, xbc, dt_raw, conv_w, conv_b, dt_bias, a_log, d_skip, ssd_norm_w)
    return jnp.concatenate([y_a, y_b, y_c], axis=-1) @ w_out


def setup_inputs(seed: int = 0) -> dict:
    key = jax.random.key(seed)
    ks = jax.random.split(key, 32)
    f32 = jnp.float32

    def nrm(i, shape, scale):
        return jax.random.normal(ks[i], shape, f32) * scale

    def gain(i, shape):
        return 1.0 + 0.02 * jax.random.normal(ks[i], shape, f32)

    D = D_MODEL
    dt0 = jnp.exp(jax.random.uniform(ks[13], (DEPTH, SSD_HEADS), f32)
                  * (math.log(0.1) - math.log(0.001)) + math.log(0.001))
    return {
        "x": nrm(0, (BATCH, SEQ, D), 1.0),
        "meta": nrm(1, (N_META, D), 1.0),
        "f1_gate": nrm(2, (DEPTH, D, D_FF), D ** -0.5),
        "f1_up": nrm(3, (DEPTH, D, D_FF), D ** -0.5),
        "f1_down": nrm(4, (DEPTH, D_FF, D), BETA * D_FF ** -0.5),
        "ln1_g": gain(5, (DEPTH, D)),
        "ln1_b": nrm(6, (DEPTH, D), 0.02),
        "w_in": nrm(7, (DEPTH, D, D_IN), D ** -0.5),
        "b_fgate": jax.random.uniform(ks[8], (DEPTH, ATTN_HEADS), f32, 1.0, 6.0),
        "pool_w": nrm(9, (DEPTH, POOL_GROUPS, POOL_GROUP_DIM, POOL_GROUP_DIM), POOL_GROUP_DIM ** -0.5),
        "pool_scale": gain(10, (DEPTH, D_POOL)),
        "conv_w": nrm(11, (DEPTH, CONV_K, D_CONV), CONV_K ** -0.5),
        "conv_b": nrm(12, (DEPTH, D_CONV), 0.02),
        "dt_bias": dt0 + jnp.log(-jnp.expm1(-dt0)),
        "a_log": jnp.log(jax.random.uniform(ks[14], (DEPTH, SSD_HEADS), f32, 1.0, 16.0)),
        "d_skip": gain(15, (DEPTH, SSD_HEADS)),
        "ssd_norm_w": gain(16, (DEPTH, D_SSD)),
        "w_out": nrm(17, (DEPTH, D_MIX, D), BETA * D_MIX ** -0.5),
        "ln2_g": gain(18, (DEPTH, D)),
        "ln2_b": nrm(19, (DEPTH, D), 0.02),
        "f2_gate": nrm(20, (DEPTH, D, D_FF), D ** -0.5),
        "f2_up": nrm(21, (DEPTH, D, D_FF), D ** -0.5),
        "f2_down": nrm(22, (DEPTH, D_FF, D), BETA * D_FF ** -0.5),
        "ln3_g": gain(23, (DEPTH, D)),
        "ln3_b": nrm(24, (DEPTH, D), 0.02),
    }


def reference(x, meta, f1_gate, f1_up, f1_down, ln1_g, ln1_b, w_in, b_fgate, pool_w,
              pool_scale, conv_w, conv_b, dt_bias, a_log, d_skip, ssd_norm_w, w_out,
              ln2_g, ln2_b, f2_gate, f2_up, f2_down, ln3_g, ln3_b):
    b = x.shape[0]
    h = jnp.concatenate([jnp.broadcast_to(meta[None].astype(x.dtype), (b, N_META, D_MODEL)), x],
                        axis=1)
    for i in range(DEPTH):
        h = layer_norm(ALPHA * h + 0.5 * swiglu(h, f1_gate[i], f1_up[i], f1_down[i]),
                       ln1_g[i], ln1_b[i])
        h = layer_norm(ALPHA * h + hybrid_mixer(h, w_in[i], b_fgate[i], pool_w[i], pool_scale[i],
                                                conv_w[i], conv_b[i], dt_bias[i], a_log[i],
                                                d_skip[i], ssd_norm_w[i], w_out[i]),
                       ln2_g[i], ln2_b[i])
        h = layer_norm(ALPHA * h + 0.5 * swiglu(h, f2_gate[i], f2_up[i], f2_down[i]),
                       ln3_g[i], ln3_b[i])
    return h[:, N_META:]
```

```python
import numpy as np
from contextlib import ExitStack
import ml_dtypes
import concourse.bass as bass
import concourse.mybir as mybir
from concourse.bass_utils import run_bass_kernel_spmd

F32 = mybir.dt.float32
BF16 = mybir.dt.bfloat16
AF = mybir.ActivationFunctionType
ALU = mybir.AluOpType

D = 2048
DFF = 5632
NFC = DFF // 128
NTOK = 2056
SEQT = 4112
LP = 4224
NBLK = 33
PADN = 112
ALPHA = 4.0 ** 0.25
LN_EPS = 1e-5
RMS_EPS = 1e-5
PASSES = [(0, 412), (412, 412), (824, 412), (1236, 412), (1648, 408)]
TPM = 412
NEG = -30000.0


class _Op:
    __slots__ = ("eng", "fn", "deps", "dma", "sig", "presem")


class Prog:
    ENGS = ("pe", "act", "dve", "pool", "sp")
    QUEUES = ("sp", "pool", "act")
    R = 8

    def __init__(self, nc, stack):
        self.nc = nc
        self.stack = stack
        self.ops = []
        self.lastw = {}
        self.rd = {}
        self.sems = {e: stack.enter_context(nc.semaphore("s_" + e)) for e in self.ENGS}
        self.dsems = {q: [stack.enter_context(nc.semaphore("d_%s%d" % (q, i))) for i in range(self.R)]
                      for q in self.QUEUES}

    def sb(self, name, shape, dt):
        return self.stack.enter_context(self.nc.sbuf_tensor("sb_" + name, shape, dt))

    def ps(self, name, shape, dt=F32):
        return self.stack.enter_context(self.nc.psum_tensor("ps_" + name, shape, dt))

    def op(self, eng, fn, r=(), w=(), dma=False):
        i = len(self.ops)
        deps = {}
        for k in r:
            p = self.lastw.get(k)
            if p is not None:
                deps[p] = "raw"
            if isinstance(k, tuple) and k[0] == "ps":
                for en, q in self.rd.get(k, {}).items():
                    if en != eng and not isinstance(q, list):
                        deps.setdefault(q, "rar")
        for k in w:
            p = self.lastw.get(k)
            if p is not None and p not in deps:
                deps[p] = "waw"
            for q in self.rd.get(k, {}).values():
                if isinstance(q, list):
                    for qq in q:
                        deps.setdefault(qq, "war")
                else:
                    deps.setdefault(q, "war")
        for k in r:
            d = self.rd.setdefault(k, {})
            if dma:
                d.setdefault("dma", []).append(i)
            else:
                d[eng] = i
        for k in w:
            self.lastw[k] = i
            self.rd[k] = {}
        o = _Op()
        o.eng, o.fn, o.deps, o.dma, o.sig, o.presem = eng, fn, deps, dma, None, None
        self.ops.append(o)
        return i

    def _need_wait(self, op, po, kind):
        if po.dma or op.dma:
            return True
        if po.eng == op.eng:
            if po.eng == "pe":
                return False
            return kind != "war"
        return True

    def emit(self):
        ops = self.ops
        need = set()
        for op in ops:
            for p, kind in op.deps.items():
                if self._need_wait(op, ops[p], kind):
                    need.add(p)
        cnt = {e: 0 for e in self.ENGS}
        dcnt = {q: 0 for q in self.QUEUES}
        for i, op in enumerate(ops):
            if op.dma:
                j = dcnt[op.eng]
                dcnt[op.eng] += 1
                s = self.dsems[op.eng][j % self.R]
                v = 16 * (j // self.R + 1)
                op.sig = (s, v)
                if j >= self.R:
                    op.presem = (s, v - 16)
            elif i in need:
                cnt[op.eng] += 1
                op.sig = (self.sems[op.eng], cnt[op.eng])
        self.final_dma = {q: dcnt[q] for q in self.QUEUES}

        def run(engname, e):
            waited = {}

            def wait(s, v):
                key = id(s)
                if waited.get(key, 0) < v:
                    e.wait_ge(s, v)
                    waited[key] = v

            for op in ops:
                if op.eng != engname:
                    continue
                for p, kind in op.deps.items():
                    po = ops[p]
                    if self._need_wait(op, po, kind):
                        wait(*po.sig)
                if op.presem is not None:
                    wait(*op.presem)
                ins = op.fn(e)
                if op.sig is not None:
                    ins.then_inc(op.sig[0], 16 if op.dma else 1)
            if engname == "sp":
                for q in self.QUEUES:
                    n = self.final_dma[q]
                    for j in range(max(0, n - self.R), n):
                        wait(self.dsems[q][j % self.R], 16 * (j // self.R + 1))

        with self.nc.Block() as block:
            @block.tensor
            def _(e):
                run("pe", e)

            @block.scalar
            def _(e):
                run("act", e)

            @block.vector
            def _(e):
                run("dve", e)

            @block.gpsimd
            def _(e):
                run("pool", e)

            @block.sync
            def _(e):
                run("sp", e)


class TokState:
    def __init__(self, P):
        self.P = P
        self.xf = P.sb("xf", [128, 16, TPM], F32)
        self.xb = P.sb("xb", [128, 16, TPM], BF16)
        self.act = P.sb("act", [128, NFC, TPM], BF16)
        self.wgu = [P.sb("wgu%d" % i, [128, 2, 16, 512], BF16) for i in range(2)]
        self.wd = [P.sb("wd%d" % i, [128, 11, 512], BF16) for i in range(2)]
        self.sg = [P.sb("sg%d" % i, [128, TPM], F32) for i in range(2)]
        self.tmp = [P.sb("tmp%d" % i, [128, TPM], F32) for i in range(2)]
        self.mean = P.sb("mean", [128, TPM], F32)
        self.msq = P.sb("msq", [128, TPM], F32)
        self.var = P.sb("var", [128, TPM], F32)
        self.rstd = P.sb("rstd", [128, TPM], F32)
        self.ones = P.sb("ones", [128, 128], BF16)
        self.lnp = P.sb("lnp", [128, 6, 16], F32)
        self.psb = [P.ps("psb%d" % i, [128, 512]) for i in range(8)]
        self.wslot = 0
        self.dslot = 0
        self.it = 0
        P.op("dve", lambda e: e.memset(self.ones[:], 1.0), w=["ones"])


def emit_ffn(P, S, wg_d, wu_d, wd_d, Tp):
    wg_r = wg_d.rearrange("(k p) f -> p k f", p=128)
    wu_r = wu_d.rearrange("(k p) f -> p k f", p=128)
    wd_r = wd_d.rearrange("(c p) d -> p c d", p=128)
    xf, xb, act, ps = S.xf, S.xb, S.act, S.psb
    xfk = [("xf", dc) for dc in range(16)]
    xbk = [("xb", dc) for dc in range(16)]

    P.op("act", lambda e: e.activation(out=xf[:, :, :Tp], in_=xf[:, :, :Tp], func=AF.Copy, scale=ALPHA),
         r=xfk, w=xfk)

    for grp in range(11):
        slot = S.wslot
        S.wslot ^= 1
        wt = S.wgu[slot]
        P.op("pool", lambda e, wt=wt, grp=grp: e.dma_start(out=wt[:, 0, :, :], in_=wg_r[:, :, grp * 512:(grp + 1) * 512]),
             w=[("wgu", slot, 0)], dma=True)
        P.op("pool", lambda e, wt=wt, grp=grp: e.dma_start(out=wt[:, 1, :, :], in_=wu_r[:, :, grp * 512:(grp + 1) * 512]),
             w=[("wgu", slot, 1)], dma=True)
        for j in range(4):
            fc = grp * 4 + j
            gb = S.it % 2
            ub = 2 + S.it % 2
            S.it += 1

            def mm(e, wt=wt, j=j, which=0, bank=gb):
                for k in range(16):
                    ins = e.matmul(ps[bank][:, :Tp], lhsT=wt[:, which, k, j * 128:(j + 1) * 128], rhs=xb[:, k, :Tp],
                                   start=(k == 0), stop=(k == 15))
                return ins

            P.op("pe", lambda e, mm=mm, gb=gb: mm(e, which=0, bank=gb), r=[("wgu", slot, 0)] + xbk, w=[("ps", gb)])
            P.op("pe", lambda e, mm=mm, ub=ub: mm(e, which=1, bank=ub), r=[("wgu", slot, 1)] + xbk, w=[("ps", ub)])
            P.op("act", lambda e, gb=gb: e.activation(out=S.sg[gb][:, :Tp], in_=ps[gb][:, :Tp], func=AF.Silu),
                 r=[("ps", gb)], w=[("sg", gb)])
            P.op("dve", lambda e, gb=gb, ub=ub, fc=fc: e.tensor_tensor(out=act[:, fc, :Tp], in0=S.sg[gb][:, :Tp],
                                                                     in1=ps[ub][:, :Tp], op=ALU.mult),
                 r=[("sg", gb), ("ps", ub)], w=[("act", fc)])

    for dcb in range(4):
        for g2 in range(4):
            slot = S.dslot
            S.dslot ^= 1
            wt = S.wd[slot]
            P.op("pool", lambda e, wt=wt, g2=g2, dcb=dcb: e.dma_start(
                out=wt[:, :, :], in_=wd_r[:, 11 * g2:11 * g2 + 11, dcb * 512:(dcb + 1) * 512]),
                w=[("wd", slot)], dma=True)
            for j in range(4):
                def mmd(e, wt=wt, g2=g2, j=j):
                    for fl in range(11):
                        fc = 11 * g2 + fl
                        ins = e.matmul(ps[4 + j][:, :Tp], lhsT=wt[:, fl, j * 128:(j + 1) * 128], rhs=act[:, fc, :Tp],
                                       start=(fc == 0), stop=(fc == NFC - 1))
                    return ins

                P.op("pe", mmd, r=[("wd", slot)] + [("act", 11 * g2 + fl) for fl in range(11)], w=[("ps", 4 + j)])
        for j in range(4):
            dc = 4 * dcb + j
            P.op("dve", lambda e, j=j, dc=dc: e.scalar_tensor_tensor(
                out=xf[:, dc, :Tp], in0=ps[4 + j][:, :Tp], scalar=0.5, in1=xf[:, dc, :Tp], op0=ALU.mult, op1=ALU.add),
                r=[("ps", 4 + j), ("xf", dc)], w=[("xf", dc)])


def emit_ln(P, S, li, Tp, write_xb=True):
    xf, xb, act, ps = S.xf, S.xb, S.act, S.psb
    xfk = [("xf", dc) for dc in range(16)]
    yb = act[:, 0:16, :Tp]
    ysq = act[:, 16:32, :Tp]
    P.op("act", lambda e: e.activation(out=yb, in_=xf[:, :, :Tp], func=AF.Copy), r=xfk,
         w=[("act", i) for i in range(16)])
    P.op("act", lambda e: e.activation(out=ysq, in_=xf[:, :, :Tp], func=AF.Square), r=xfk,
         w=[("act", i) for i in range(16, 32)])

    def mms(e, base, bank):
        for dc in range(16):
            ins = e.matmul(ps[bank][:, :Tp], lhsT=S.ones[:, :], rhs=act[:, base + dc, :Tp], start=(dc == 0), stop=(dc == 15))
        return ins

    P.op("pe", lambda e: mms(e, 0, 4), r=["ones"] + [("act", i) for i in range(16)], w=[("ps", 4)])
    P.op("pe", lambda e: mms(e, 16, 5), r=["ones"] + [("act", i) for i in range(16, 32)], w=[("ps", 5)])
    P.op("dve", lambda e: e.tensor_scalar(out=S.mean[:, :Tp], in0=ps[4][:, :Tp], scalar1=1.0 / D, scalar2=None, op0=ALU.mult),
         r=[("ps", 4)], w=["mean"])
    P.op("dve", lambda e: e.tensor_tensor(out=S.msq[:, :Tp], in0=S.mean[:, :Tp], in1=S.mean[:, :Tp], op=ALU.mult),
         r=["mean"], w=["msq"])
    P.op("dve", lambda e: e.scalar_tensor_tensor(out=S.var[:, :Tp], in0=ps[5][:, :Tp], scalar=1.0 / D, in1=S.msq[:, :Tp],
                                                 op0=ALU.mult, op1=ALU.subtract), r=[("ps", 5), "msq"], w=["var"])
    P.op("dve", lambda e: e.tensor_scalar(out=S.var[:, :Tp], in0=S.var[:, :Tp], scalar1=LN_EPS, scalar2=None, op0=ALU.add),
         r=["var"], w=["var"])
    P.op("act", lambda e: e.activation(out=S.msq[:, :Tp], in_=S.var[:, :Tp], func=AF.Sqrt), r=["var"], w=["msq"])
    P.op("dve", lambda e: e.reciprocal(out=S.rstd[:, :Tp], in_=S.msq[:, :Tp]), r=["msq"], w=["rstd"])
    mb = S.mean[:, :Tp].unsqueeze(1).to_broadcast([128, 16, Tp])
    rb = S.rstd[:, :Tp].unsqueeze(1).to_broadcast([128, 16, Tp])
    P.op("dve", lambda e: e.tensor_tensor(out=xf[:, :, :Tp], in0=xf[:, :, :Tp], in1=mb, op=ALU.subtract),
         r=xfk + ["mean"], w=xfk)
    P.op("dve", lambda e: e.tensor_tensor(out=xf[:, :, :Tp], in0=xf[:, :, :Tp], in1=rb, op=ALU.mult),
         r=xfk + ["rstd"], w=xfk)
    for dc in range(16):
        g = S.lnp[:, 2 * li, dc:dc + 1]
        b = S.lnp[:, 2 * li + 1, dc:dc + 1]
        P.op("act", lambda e, dc=dc, g=g, b=b: e.activation(out=xf[:, dc, :Tp], in_=xf[:, dc, :Tp], func=AF.Identity,
                                                            scale=g, bias=b), r=[("xf", dc), "lnp"], w=[("xf", dc)])
        if write_xb:
            P.op("act", lambda e, dc=dc: e.activation(out=xb[:, dc, :Tp], in_=xf[:, dc, :Tp], func=AF.Copy),
                 r=[("xf", dc)], w=[("xb", dc)])


def emit_outproj(P, S, wo_d, Tp):
    wo_r = wo_d.rearrange("(k p) f -> p k f", p=128)
    xf, xb, ps = S.xf, S.xb, S.psb
    xfk = [("xf", dc) for dc in range(16)]
    xbk = [("xb", dc) for dc in range(16)]
    P.op("act", lambda e: e.activation(out=xf[:, :, :Tp], in_=xf[:, :, :Tp], func=AF.Copy, scale=ALPHA),
         r=xfk, w=xfk)
    for dcb in range(4):
        slot = S.wslot
        S.wslot ^= 1
        wt = S.wgu[slot]
        P.op("pool", lambda e, wt=wt, dcb=dcb: e.dma_start(out=wt[:, 0, :, :], in_=wo_r[:, :, dcb * 512:(dcb + 1) * 512]),
             w=[("wgu", slot, 0)], dma=True)
        for j in range(4):
            dc = 4 * dcb + j

            def mm(e, wt=wt, j=j):
                for k in range(16):
                    ins = e.matmul(ps[4 + j][:, :Tp], lhsT=wt[:, 0, k, j * 128:(j + 1) * 128], rhs=xb[:, k, :Tp],
                                   start=(k == 0), stop=(k == 15))
                return ins

            P.op("pe", mm, r=[("wgu", slot, 0)] + xbk, w=[("ps", 4 + j)])
            P.op("dve", lambda e, j=j, dc=dc: e.tensor_tensor(out=xf[:, dc, :Tp], in0=ps[4 + j][:, :Tp], in1=xf[:, dc, :Tp],
                                                              op=ALU.add), r=[("ps", 4 + j), ("xf", dc)], w=[("xf", dc)])


def build_tok_program(kind):
    nc = bass.Bass("TRN2", target_bir_lowering=False)
    xin = nc.dram_tensor("xin", [D, NTOK], F32, kind="ExternalInput").ap()
    lnp = nc.dram_tensor("lnp", [128, 6, 16], F32, kind="ExternalInput").ap()
    wg = nc.dram_tensor("wg", [D, DFF], F32, kind="ExternalInput").ap()
    wu = nc.dram_tensor("wu", [D, DFF], F32, kind="ExternalInput").ap()
    wd = nc.dram_tensor("wd", [DFF, D], F32, kind="ExternalInput").ap()
    if kind == "C":
        yin = nc.dram_tensor("yin", [D, NTOK], BF16, kind="ExternalInput").ap()
        wo = nc.dram_tensor("wo", [D, D], F32, kind="ExternalInput").ap()
    hout = nc.dram_tensor("hout", [D, NTOK], F32, kind="ExternalOutput").ap()
    xin_r = xin.rearrange("(c p) t -> p c t", p=128)
    hout_r = hout.rearrange("(c p) t -> p c t", p=128)
    with ExitStack() as stack:
        P = Prog(nc, stack)
        S = TokState(P)
        xfk = [("xf", dc) for dc in range(16)]
        xbk = [("xb", dc) for dc in range(16)]
        P.op("sp", lambda e: e.dma_start(out=S.lnp[:], in_=lnp), w=["lnp"], dma=True)
        for (t0, Tp) in PASSES:
            P.op("sp", lambda e, t0=t0, Tp=Tp: e.dma_start(out=S.xf[:, :, :Tp], in_=xin_r[:, :, t0:t0 + Tp]), w=xfk, dma=True)
            if kind == "A":
                P.op("act", lambda e, Tp=Tp: e.activation(out=S.xb[:, :, :Tp], in_=S.xf[:, :, :Tp], func=AF.Copy),
                     r=xfk, w=xbk)
                emit_ffn(P, S, wg, wu, wd, Tp)
                emit_ln(P, S, 0, Tp, write_xb=False)
            else:
                yin_r = yin.rearrange("(c p) t -> p c t", p=128)
                P.op("sp", lambda e, t0=t0, Tp=Tp: e.dma_start(out=S.xb[:, :, :Tp], in_=yin_r[:, :, t0:t0 + Tp]), w=xbk, dma=True)
                emit_outproj(P, S, wo, Tp)
                emit_ln(P, S, 1, Tp, write_xb=True)
                emit_ffn(P, S, wg, wu, wd, Tp)
                emit_ln(P, S, 2, Tp, write_xb=False)
            P.op("sp", lambda e, t0=t0, Tp=Tp: e.dma_start(out=hout_r[:, :, t0:t0 + Tp], in_=S.xf[:, :, :Tp]), r=xfk, dma=True)
        P.emit()
    return nc


NSP = 72
BIGW = 39 * 1024
TILES = [(i * 512, min(512, LP - i * 512)) for i in range(9)]


def build_mix_program(phases=("attn", "pool", "ssd"), attn_heads=4, attn_qg=9):
    nc = bass.Bass("TRN2", target_bir_lowering=False)
    hin = nc.dram_tensor("hin", [D, SEQT], F32, kind="ExternalInput").ap()
    w_at = nc.dram_tensor("w_at", [D, 772], F32, kind="ExternalInput").ap()
    w_pl = nc.dram_tensor("w_pl", [D, 256], F32, kind="ExternalInput").ap()
    w_sd = nc.dram_tensor("w_sd", [D, 1288], F32, kind="ExternalInput").ap()
    spd = nc.dram_tensor("spd", [128, NSP], F32, kind="ExternalInput").ap()
    nwd = nc.dram_tensor("nwd", [128, 512], F32, kind="ExternalInput").ap()
    rcd = nc.dram_tensor("rcd", [2, 128, LP], F32, kind="ExternalInput").ap()
    pwd = nc.dram_tensor("pwd", [2, 128, 128], F32, kind="ExternalInput").ap()
    yout = nc.dram_tensor("yout", [1024, SEQT], BF16, kind="ExternalOutput").ap()
    hin_r = hin.rearrange("(k p) t -> p k t", p=128)

    with ExitStack() as stack:
        P = Prog(nc, stack)
        ps = [P.ps("b%d" % i, [128, 512]) for i in range(8)]
        psb = [p[:].bitcast(BF16) for p in ps]
        hT = [P.sb("hT%d" % i, [128, 16, 512], BF16) for i in range(2)]
        spt = P.sb("spt", [128, NSP], F32)
        cst = P.sb("cst", [128, 8], F32)
        ident = P.sb("ident", [128, 128], BF16)
        mask = P.sb("mask", [128, 4, 512], BF16)
        onesf = P.sb("onesf", [128, 128], F32)
        big = P.sb("big", [128, BIGW], F32)

        P.op("sp", lambda e: e.dma_start(out=spt[:], in_=spd), w=["spt"], dma=True)
        P.op("dve", lambda e: e.memset(cst[:, 0:1], 1.0), w=["cst"])
        P.op("dve", lambda e: e.memset(cst[:, 1:2], 1e-30), w=["cst"])
        P.op("dve", lambda e: e.memset(cst[:, 2:3], RMS_EPS), w=["cst"])
        P.op("dve", lambda e: e.memset(cst[:, 3:4], 0.0), w=["cst"])
        P.op("dve", lambda e: e.memset(onesf[:], 1.0), w=["onesf"])
        P.op("dve", lambda e: e.memset(ident[:], 0.0), w=["ident"])
        P.op("pool", lambda e: e.affine_select(out=ident[:], in_=ident[:], pattern=[[1, 128]], compare_op=ALU.not_equal,
                                               fill=1.0, base=0, channel_multiplier=-1), r=["ident"], w=["ident"])
        P.op("dve", lambda e: e.memset(mask[:], 0.0), w=["mask"])
        for j in range(4):
            P.op("pool", lambda e, j=j: e.affine_select(out=mask[:, j, :], in_=mask[:, j, :], pattern=[[1, 512]],
                                                        compare_op=ALU.is_ge, fill=NEG, base=-128 * j,
                                                        channel_multiplier=-1), r=["mask"], w=["mask"])

        hslot = [0]

        def load_h(ti):
            p0, n = TILES[ti]
            s = hslot[0]
            hslot[0] ^= 1
            t0 = p0 - PADN
            if ti == 0:
                P.op("dve", lambda e, s=s: e.memset(hT[s][:, :, 0:PADN], 0.0), w=[("hT", s)])
                P.op("pool", lambda e, s=s: e.dma_start(out=hT[s][:, :, PADN:512], in_=hin_r[:, :, 0:512 - PADN]),
                     w=[("hT", s)], dma=True)
            else:
                P.op("pool", lambda e, s=s, t0=t0, n=n: e.dma_start(out=hT[s][:, :, :n], in_=hin_r[:, :, t0:t0 + n]),
                     w=[("hT", s)], dma=True)
            return s, n

        pcnt = [0]

        def bank():
            b = pcnt[0] % 8
            pcnt[0] += 1
            return b

        def proj_fm(s, n, wt, c0, m, b):
            def f(e):
                for k in range(16):
                    ins = e.matmul(ps[b][0:m, :n], lhsT=wt[:, k, c0:c0 + m], rhs=hT[s][:, k, :n], start=(k == 0), stop=(k == 15))
                return ins
            return f

        def proj_tm(s, blk, wt, c0, ncol, b):
            def f(e):
                for k in range(16):
                    ins = e.matmul(ps[b][:, :ncol], lhsT=hT[s][:, k, blk * 128:(blk + 1) * 128], rhs=wt[:, k, c0:c0 + ncol],
                                   start=(k == 0), stop=(k == 15))
                return ins
            return f

        o = 0
        wat = big[:, o:o + 16 * 772 // 2].bitcast(BF16).rearrange("p (k c) -> p k c", k=16); o += 16 * 772 // 2
        qa, ka = [], []
        for h in range(4):
            qa.append(big[:, o:o + LP // 2].bitcast(BF16)); o += LP // 2
            ka.append(big[:, o:o + LP // 2].bitcast(BF16)); o += LP // 2
        vaug = big[:, o:o + NBLK * 4 * 66 // 2].bitcast(BF16).rearrange("p (b h d) -> p b h d", b=NBLK, h=4); o += NBLK * 4 * 66 // 2
        scrf = big[:, o:o + LP]; o += LP
        scrb = big[:, o:o + LP // 2].bitcast(BF16); o += LP // 2
        pT = [big[:, o + i * 256:o + (i + 1) * 256].bitcast(BF16) for i in range(3)]; o += 3 * 256
        rden = big[:, o:o + 512]; o += 512
        bcs = big[:, o:o + 512]; o += 512
        yat = [big[:, o + i * 256:o + (i + 1) * 256].bitcast(BF16) for i in range(2)]; o += 2 * 256
        onesb = big[:, o:o + LP // 2].bitcast(BF16); o += LP // 2
        assert o <= BIGW, o
        w_at_r = w_at.rearrange("(k p) c -> p k c", p=128)
        if "attn" in phases:
            P.op("pool", lambda e: e.dma_start(out=wat, in_=w_at_r), w=["wat"], dma=True)
        P.op("dve", lambda e: e.memset(vaug[:, :, :, 64:65], 1.0), w=["vaug1"])
        P.op("dve", lambda e: e.memset(onesb[0:4, :], 1.0), w=["onesb"])
        for h in range(4):
            P.op("dve", lambda e, h=h: e.memset(qa[h][64:70, :], 1.0), w=[("qa", h, "aug")])
            P.op("dve", lambda e, h=h: e.memset(ka[h][64:70, :], 1.0), w=[("ka", h, "aug")])
            P.op("dve", lambda e, h=h: e.memset(ka[h][64:67, :], -1.0), w=[("ka", h, "aug")])
        P.op("dve", lambda e: e.tensor_scalar(out=cst[0:4, 4:5], in0=spt[0:4, 64:65], scalar1=-1.0, scalar2=None, op0=ALU.mult),
             r=["spt"], w=["negb"])

        if "attn" in phases:
            for ti in range(9):
                s, n = load_h(ti)
                p0 = TILES[ti][0]
                for h in range(4):
                    b = bank()
                    P.op("pe", proj_fm(s, n, wat, h * 64, 64, b), r=["wat", ("hT", s)], w=[("ps", b)])
                    P.op("act", lambda e, h=h, b=b, p0=p0, n=n: e.activation(out=qa[h][0:64, p0:p0 + n], in_=ps[b][0:64, :n],
                                                                             func=AF.Copy, scale=0.125),
                         r=[("ps", b)], w=[("qa", h, ti)])
                    b = bank()
                    P.op("pe", proj_fm(s, n, wat, 256 + h * 64, 64, b), r=["wat", ("hT", s)], w=[("ps", b)])
                    P.op("act", lambda e, h=h, b=b, p0=p0, n=n: e.activation(out=ka[h][0:64, p0:p0 + n], in_=ps[b][0:64, :n],
                                                                             func=AF.Copy),
                         r=[("ps", b)], w=[("ka", h, ti)])
                b = bank()
                P.op("pe", proj_fm(s, n, wat, 768, 4, b), r=["wat", ("hT", s)], w=[("ps", b)])
                P.op("act", lambda e, b=b, p0=p0, n=n: e.activation(out=scrf[64:68, p0:p0 + n], in_=ps[b][0:4, :n], func=AF.Exp,
                                                                    scale=-1.0, bias=cst[0:4, 4:5]),
                     r=[("ps", b), "negb"], w=[("e1", ti)])
                P.op("act", lambda e, p0=p0, n=n: e.activation(out=scrf[0:4, p0:p0 + n], in_=scrf[64:68, p0:p0 + n], func=AF.Ln,
                                                               bias=cst[64:68, 0:1]),
                     r=[("e1", ti), "cst"], w=[("nlf", ti)])
                for blk in range(n // 128):
                    bg = p0 // 128 + blk
                    b = bank()
                    P.op("pe", proj_tm(s, blk, wat, 512, 256, b), r=["wat", ("hT", s)], w=[("ps", b)])
                    P.op("act", lambda e, b=b, bg=bg: e.activation(out=vaug[:, bg, :, 0:64],
                                                                   in_=ps[b][:, 0:256].rearrange("p (h d) -> p h d", h=4),
                                                                   func=AF.Copy),
                         r=[("ps", b)], w=[("v", bg)])
            nlfk = [("nlf", ti) for ti in range(9)]
            P.op("dve", lambda e: e.memset(scrf[0:4, 0:PADN], 0.0), r=nlfk, w=nlfk)
            P.op("dve", lambda e: e.tensor_tensor_scan(out=scrf[0:4, :], data0=onesb[0:4, :], data1=scrf[0:4, :], initial=0.0,
                                                       op0=ALU.mult, op1=ALU.add), r=nlfk + ["onesb"], w=["cpos"] + nlfk)
            P.op("dve", lambda e: e.tensor_copy(out=scrb[0:4, :], in_=scrf[0:4, :]), r=["cpos"], w=["hi"])
            P.op("dve", lambda e: e.tensor_tensor(out=scrf[32:36, :], in0=scrf[0:4, :], in1=scrb[0:4, :], op=ALU.subtract),
                 r=["cpos", "hi"], w=["r1"])
            P.op("dve", lambda e: e.tensor_copy(out=scrb[32:36, :], in_=scrf[32:36, :]), r=["r1"], w=["mid"])
            P.op("dve", lambda e: e.tensor_tensor(out=scrf[32:36, :], in0=scrf[32:36, :], in1=scrb[32:36, :], op=ALU.subtract),
                 r=["r1", "mid"], w=["r1"])
            P.op("dve", lambda e: e.tensor_copy(out=scrb[64:68, :], in_=scrf[32:36, :]), r=["r1"], w=["lo"])
            P.op("dve", lambda e: e.tensor_copy(out=scrb[96:100, :], in_=scrb[0:4, :]), r=["hi"], w=["hik"])
            P.op("dve", lambda e: e.memset(scrb[96:100, 0:PADN], NEG), r=["hik"], w=["hik"])
            for h in range(4):
                for j, (row, key) in enumerate(((0, "hi"), (32, "mid"), (64, "lo"))):
                    P.op("sp", lambda e, h=h, j=j, row=row: e.dma_start(out=qa[h][64 + j:65 + j, :], in_=scrb[row + h:row + h + 1, :]),
                         r=[key, ("qa", h, "aug")], w=[("qa", h, "aug%d" % j)], dma=True)
                for j, (row, key) in enumerate(((96, "hik"), (32, "mid"), (64, "lo"))):
                    P.op("sp", lambda e, h=h, j=j, row=row: e.dma_start(out=ka[h][67 + j:68 + j, :], in_=scrb[row + h:row + h + 1, :]),
                         r=[key, ("ka", h, "aug")], w=[("ka", h, "aug%d" % j)], dma=True)

        pti = [0]
        for h in range(attn_heads if "attn" in phases else 0):
            qk_r = [("qa", h, t) for t in range(9)] + [("qa", h, "aug%d" % j) for j in range(3)] + [("qa", h, "aug")]
            kk_r = [("ka", h, t) for t in range(9)] + [("ka", h, "aug%d" % j) for j in range(3)] + [("ka", h, "aug")]
            for qg in range(attn_qg):
                q0, nq = TILES[qg]
                nkb = 4 * qg + nq // 128
                bo = 4 + qg % 2
                for kb in range(nkb):
                    bs = (kb % 2)
                    diag = kb >= 4 * qg

                    def sc(e, h=h, kb=kb, q0=q0, nq=nq, bs=bs, diag=diag, qg=qg):
                        ins = e.matmul(ps[bs][:, :nq], lhsT=ka[h][0:70, kb * 128:(kb + 1) * 128], rhs=qa[h][0:70, q0:q0 + nq],
                                       start=True, stop=not diag)
                        if diag:
                            ins = e.matmul(ps[bs][:, :nq], lhsT=ident[:, :], rhs=mask[:, kb - 4 * qg, :nq], start=False, stop=True)
                        return ins

                    P.op("pe", sc, r=qk_r + kk_r + ["ident", "mask"], w=[("ps", bs)])
                    pi = pti[0] % 3
                    pti[0] += 1
                    P.op("act", lambda e, bs=bs, pi=pi, nq=nq: e.activation(out=pT[pi][:, :nq], in_=ps[bs][:, :nq], func=AF.Exp),
                         r=[("ps", bs)], w=[("pT", pi)])
                    P.op("pe", lambda e, h=h, kb=kb, pi=pi, nq=nq, bo=bo, nkb=nkb: e.matmul(
                        ps[bo][0:65, :nq], lhsT=vaug[:, kb, h, 0:65], rhs=pT[pi][:, :nq], start=(kb == 0), stop=(kb == nkb - 1)),
                        r=[("v", kb), "vaug1", ("pT", pi)], w=[("ps", bo)])
                P.op("dve", lambda e, bo=bo, nq=nq: e.tensor_scalar(out=rden[64:65, :nq], in0=ps[bo][64:65, :nq], scalar1=1e-30,
                                                                    scalar2=None, op0=ALU.add), r=[("ps", bo)], w=["rden"])
                P.op("dve", lambda e, nq=nq: e.reciprocal(out=rden[64:65, :nq], in_=rden[64:65, :nq]), r=["rden"], w=["rden"])
                P.op("pe", lambda e, nq=nq: e.matmul(ps[6][0:64, :nq], lhsT=onesf[64:65, 0:64], rhs=rden[64:65, :nq], start=True, stop=True),
                     r=["rden", "onesf"], w=[("ps", 6)])
                P.op("act", lambda e, nq=nq: e.activation(out=bcs[0:64, :nq], in_=ps[6][0:64, :nq], func=AF.Copy),
                     r=[("ps", 6)], w=["bcs"])
                yi = (h * 9 + qg) % 2
                P.op("dve", lambda e, bo=bo, nq=nq, yi=yi: e.tensor_tensor(out=yat[yi][0:64, :nq], in0=ps[bo][0:64, :nq],
                                                                           in1=bcs[0:64, :nq], op=ALU.mult),
                     r=[("ps", bo), "bcs"], w=[("yat", yi)])
                lo = PADN if qg == 0 else 0
                P.op("sp", lambda e, h=h, yi=yi, q0=q0, nq=nq, lo=lo: e.dma_start(
                    out=yout[h * 64:(h + 1) * 64, q0 + lo - PADN:q0 + nq - PADN], in_=yat[yi][0:64, lo:nq]),
                    r=[("yat", yi)], dma=True)

        nlfk = [("nlf", ti) for ti in range(9)]
        ARENA1 = (["wat", "vaug1", "onesb", "cpos", "hi", "mid", "lo", "hik", "r1", "rden", "bcs", "negb"]
                  + nlfk + [("e1", t) for t in range(9)]
                  + [("qa", h, t) for h in range(4) for t in list(range(9)) + ["aug", "aug0", "aug1", "aug2"]]
                  + [("ka", h, t) for h in range(4) for t in list(range(9)) + ["aug", "aug0", "aug1", "aug2"]]
                  + [("v", b) for b in range(NBLK)] + [("pT", i) for i in range(3)] + [("yat", i) for i in range(2)])
        P.op("dve", lambda e: e.memset(cst[:, 5:6], 0.0), r=ARENA1, w=ARENA1 + ["arena"])

        if "pool" in phases:
            build_pool_phase(P, nc, ps, hT, load_h, bank, proj_fm, big, spt, w_pl, rcd, pwd, yout)
        else:
            P.op("dve", lambda e: e.memset(cst[:, 5:6], 0.0), r=["arena"], w=["arena2"])
        if "ssd" in phases:
            build_ssd_phase(P, nc, ps, psb, hT, load_h, bank, proj_fm, proj_tm, big, spt, cst, ident, mask, onesf, w_sd, nwd, yout)
        P.emit()
    return nc


def build_pool_phase(P, nc, ps, hT, load_h, bank, proj_fm, big, spt, w_pl, rcd, pwd, yout):
    A = ["arena"]
    o = 0
    wpl = big[:, o:o + 2048].bitcast(BF16).rearrange("p (k c) -> p k c", k=16); o += 2048
    u = [big[:, o + i * LP:o + (i + 1) * LP] for i in range(2)]; o += 2 * LP
    sA = big[:, o:o + LP]; o += LP
    sB = big[:, o:o + LP]; o += LP
    rc = big[:, o:o + LP]; o += LP
    pm = big[:, o:o + LP // 2].bitcast(BF16); o += LP // 2
    pw = big[:, o:o + 128].bitcast(BF16); o += 128
    ybt = [big[:, o + i * 256:o + (i + 1) * 256].bitcast(BF16) for i in range(2)]; o += 512
    assert o <= BIGW
    P.op("pool", lambda e: e.dma_start(out=wpl, in_=w_pl.rearrange("(k p) c -> p k c", p=128)), r=A, w=["wpl"], dma=True)
    P.op("pool", lambda e: e.dma_start(out=pw.rearrange("p (g d) -> p g d", g=2), in_=pwd.rearrange("g c d -> c g d")),
         r=A, w=["pw"], dma=True)
    P.op("dve", lambda e: e.memset(sA[:, 0:16], 0.0), r=A, w=["sA"])
    P.op("dve", lambda e: e.memset(sB[:, 0:16], 0.0), r=A, w=["sB"])
    P.op("dve", lambda e: e.memset(pm[:, 0:16], 0.0), r=A, w=["pm"])
    for ti in range(9):
        s, n = load_h(ti)
        p0 = TILES[ti][0]
        for i in range(2):
            b = bank()
            P.op("pe", proj_fm(s, n, wpl, i * 128, 128, b), r=["wpl", ("hT", s)], w=[("ps", b)])
            P.op("act", lambda e, i=i, b=b, p0=p0, n=n: e.activation(out=u[i][:, p0:p0 + n], in_=ps[b][:, :n], func=AF.Copy),
                 r=[("ps", b)] + A, w=[("u", i)])
    yk = 0
    for i in range(2):
        P.op("sp", lambda e, i=i: e.dma_start(out=rc, in_=rcd[i]), r=A, w=["rc"], dma=True)
        src = u[i]
        bufs = [sA, sB, sA, sB]
        keys = ["sA", "sB", "sA", "sB"]
        skey = ("u", i)
        for st, sh in enumerate((1, 2, 4, 8)):
            dst = bufs[st]
            P.op("dve", lambda e, src=src, dst=dst, sh=sh, st=st, i=i: e.scalar_tensor_tensor(
                out=dst[:, 16:LP], in0=src[:, 16 - sh:LP - sh], scalar=spt[:, i * 4 + st:i * 4 + st + 1], in1=src[:, 16:LP],
                op0=ALU.mult, op1=ALU.add), r=[skey, "spt"], w=[keys[st]])
            src, skey = dst, keys[st]
        P.op("dve", lambda e: e.tensor_tensor(out=sA[:, 16:LP], in0=sB[:, 16:LP], in1=rc[:, 16:LP], op=ALU.mult),
             r=["sB", "rc"], w=["sA"])
        P.op("dve", lambda e, i=i: e.tensor_tensor(out=pm[:, 16:LP], in0=sA[:, 16:LP], in1=u[i][:, 16:LP], op=ALU.subtract),
             r=["sA", ("u", i)], w=["pm"])
        for ti in range(9):
            p0, n = TILES[ti]
            b = bank()
            P.op("pe", lambda e, i=i, b=b, p0=p0, n=n: e.matmul(ps[b][:, :n], lhsT=pw[:, i * 128:(i + 1) * 128], rhs=pm[:, p0:p0 + n],
                                                                start=True, stop=True), r=["pw", "pm"], w=[("ps", b)])
            k = yk % 2
            yk += 1
            P.op("act", lambda e, i=i, b=b, n=n, k=k: e.activation(out=ybt[k][:, :n], in_=ps[b][:, :n], func=AF.Copy,
                                                                   scale=spt[:, 8 + i:9 + i]), r=[("ps", b), "spt"] + A, w=[("ybt", k)])
            lo = PADN if ti == 0 else 0
            P.op("sp", lambda e, i=i, k=k, p0=p0, n=n, lo=lo: e.dma_start(
                out=yout[256 + i * 128:256 + (i + 1) * 128, p0 + lo - PADN:p0 + n - PADN], in_=ybt[k][:, lo:n]),
                r=[("ybt", k)], dma=True)
    AR = ["wpl", "pw", "sA", "sB", "pm", "rc", ("u", 0), ("u", 1), ("ybt", 0), ("ybt", 1)]
    P.op("dve", lambda e: e.memset(sA[:, 0:1], 0.0), r=AR + A, w=AR + ["arena2"])


def build_ssd_phase(P, nc, ps, psb, hT, load_h, bank, proj_fm, proj_tm, big, spt, cst, ident, mask, onesf, w_sd, nwd, yout):
    A = ["arena2"]
    o = 0
    xs_tok = big[:, o:o + NBLK * 256].bitcast(BF16).rearrange("p (b c) -> p b c", b=NBLK); o += NBLK * 256
    zs = big[:, o:o + NBLK * 256].bitcast(BF16).rearrange("p (b c) -> p b c", b=NBLK); o += NBLK * 256
    B_tok = big[:, o:o + NBLK * 64].bitcast(BF16).rearrange("p (b c) -> p b c", b=NBLK); o += NBLK * 64
    BT = big[:, o:o + LP // 2].bitcast(BF16); o += LP // 2
    CT = big[:, o:o + LP // 2].bitcast(BF16); o += LP // 2
    dt_tok = big[:, o:o + NBLK * 8].rearrange("p (b r) -> p b r", b=NBLK); o += NBLK * 8
    nwt = big[:, o:o + 512]; o += 512
    X0 = o
    wsd = big[:, o:o + 16 * 1288 // 2].bitcast(BF16).rearrange("p (k c) -> p k c", k=16); o += 16 * 1288 // 2
    xr = big[:, o:o + 6 * 516 // 2].bitcast(BF16).rearrange("p (c t) -> p c t", c=6); o += 6 * 516 // 2
    xc = big[:, o:o + 6 * 256].bitcast(BF16).rearrange("p (c t) -> p c t", c=6); o += 6 * 256
    dg = big[:, o:o + 24 * 64].bitcast(BF16).rearrange("p (c t) -> p c t", c=24); o += 24 * 64
    assert o <= BIGW, o
    o = X0
    G = big[:, o:o + 512].bitcast(BF16).rearrange("p (r l) -> p r l", r=8); o += 512
    dec = big[:, o:o + 1024]; o += 1024
    xdt = big[:, o:o + 256].bitcast(BF16); o += 256
    X2 = big[:, o:o + 256].bitcast(BF16); o += 256
    yo = big[:, o:o + 512]; o += 512
    y1 = big[:, o:o + 512]; o += 512
    gy = big[:, o:o + 512]; o += 512
    sq = big[:, o:o + 512]; o += 512
    yn = big[:, o:o + 256].bitcast(BF16); o += 256
    ycT = [big[:, o + i * 256:o + (i + 1) * 256].bitcast(BF16) for i in range(2)]; o += 512
    acsT = big[:, o:o + 256]; o += 256
    tmpH = big[:, o:o + 512]; o += 512
    H = big[:, o:o + 512]; o += 512
    Hb = big[:, o:o + 256].bitcast(BF16); o += 256
    dI = big[:, o:o + 512].bitcast(BF16).rearrange("p (r l) -> p r l", r=8); o += 512
    SEL = big[:, o:o + 1024].rearrange("p (r l) -> p r l", r=8); o += 1024
    TRI = big[:, o:o + 128]; o += 128
    LSEL = big[:, o:o + 128]; o += 128
    abc = big[:, o:o + 8]; o += 8
    sm = big[:, o:o + 8]; o += 8
    ablk = big[:, o:o + 264].rearrange("p (b r) -> p b r", b=NBLK); o += 264
    acs = big[:, o:o + 264].rearrange("p (b r) -> p b r", b=NBLK); o += 264
    cd = big[:, o:o + 264].rearrange("p (b r) -> p b r", b=NBLK); o += 264
    dst = big[:, o:o + 264].rearrange("p (b r) -> p b r", b=NBLK); o += 264
    Et = big[:, o:o + 264].rearrange("p (b r) -> p b r", b=NBLK); o += 264
    w1 = big[:, o:o + 264].rearrange("p (b r) -> p b r", b=NBLK); o += 264
    assert o <= BIGW, o

    P.op("pool", lambda e: e.dma_start(out=wsd, in_=w_sd.rearrange("(k p) c -> p k c", p=128)), r=A, w=["wsd"], dma=True)
    P.op("sp", lambda e: e.dma_start(out=nwt, in_=nwd), r=A, w=["nwt"], dma=True)
    for c in range(6):
        for j in range(4):
            P.op("dve", lambda e, c=c, j=j: e.tensor_scalar(out=dg[:, c * 4 + j, :], in0=ident[:, :],
                                                            scalar1=spt[:, 16 + c * 4 + j:17 + c * 4 + j], scalar2=None, op0=ALU.mult),
                 r=["ident", "spt"] + A, w=["dg"])
    P.op("dve", lambda e: e.memset(xr[:, :, 0:3], 0.0), r=A, w=["xr"])

    import os as _os
    _sub = int(_os.environ.get("SSD_SUB", "31"))
    for ti in range(9):
        s, n = load_h(ti)
        p0 = TILES[ti][0]
        for c in range(6):
            b = bank()
            P.op("pe", proj_fm(s, n, wsd, 512 + c * 128, 128, b), r=["wsd", ("hT", s)], w=[("ps", b)])
            P.op("act", lambda e, c=c, b=b, n=n: e.activation(out=xr[:, c, 3:3 + n], in_=ps[b][:, :n], func=AF.Copy),
                 r=[("ps", b)], w=["xr"])
        for c in range(6 if _sub & 1 else 0):
            b = bank()

            def cv(e, c=c, b=b, n=n):
                for j in range(4):
                    ins = e.matmul(ps[b][:, :n], lhsT=dg[:, c * 4 + j, :], rhs=xr[:, c, j:j + n], start=(j == 0), stop=(j == 3))
                return ins

            P.op("pe", cv, r=["dg", "xr"], w=[("ps", b)])
            if c < 4:
                dstap, key = xc[:, c, :n], ("xc", c)
            elif c == 4:
                dstap, key = BT[:, p0:p0 + n], ("BT", ti)
            else:
                dstap, key = CT[:, p0:p0 + n], ("CT", ti)
            P.op("act", lambda e, c=c, b=b, n=n, dstap=dstap: e.activation(out=dstap, in_=ps[b][:, :n], func=AF.Silu,
                                                                           bias=spt[:, 10 + c:11 + c]),
                 r=[("ps", b), "spt"] + A, w=[key])
        if ti == 0:
            P.op("dve", lambda e: e.memset(xc[:, 0:4, 0:PADN], 0.0), r=[("xc", c) for c in range(4)], w=[("xc", c) for c in range(4)])
            P.op("dve", lambda e: e.memset(BT[:, 0:PADN], 0.0), r=[("BT", 0)], w=[("BT", 0)])
            P.op("dve", lambda e: e.memset(CT[:, 0:PADN], 0.0), r=[("CT", 0)], w=[("CT", 0)])
        if n == 512 and (_sub & 16):
            P.op("dve", lambda e: e.tensor_copy(out=xr[:, :, 0:3], in_=xr[:, :, 512:515]), r=["xr"], w=["xr"])
        for blk in range(n // 128):
            bg = p0 // 128 + blk
            if not (_sub & 2):
                continue
            b = bank()

            def tr(e, b=b, blk=blk, bg=bg):
                for j in range(4):
                    ins = e.transpose(psb[b][:, j * 128:(j + 1) * 128], xc[:, j, blk * 128:(blk + 1) * 128], ident[:, :])
                ins = e.transpose(psb[b][:, 512:640], BT[:, bg * 128:(bg + 1) * 128], ident[:, :])
                return ins

            P.op("pe", tr, r=[("xc", c) for c in range(4)] + [("BT", ti), "ident"], w=[("ps", b)])
            P.op("act", lambda e, b=b, bg=bg: e.activation(out=xs_tok[:, bg, :], in_=psb[b][:, 0:512], func=AF.Copy),
                 r=[("ps", b)] + A, w=[("xs", bg)])
            P.op("dve", lambda e, b=b, bg=bg: e.tensor_copy(out=B_tok[:, bg, :], in_=psb[b][:, 512:640]),
                 r=[("ps", b)] + A, w=[("Bt", bg)])
            if not (_sub & 4):
                continue
            b = bank()
            P.op("pe", proj_tm(s, blk, wsd, 0, 512, b), r=["wsd", ("hT", s)], w=[("ps", b)])
            P.op("act", lambda e, b=b, bg=bg: e.activation(out=zs[:, bg, :], in_=ps[b][:, :], func=AF.Silu),
                 r=[("ps", b)] + A, w=[("zs", bg)])
            if not (_sub & 8):
                continue
            b = bank()
            P.op("pe", proj_tm(s, blk, wsd, 1224, 64, b), r=["wsd", ("hT", s)], w=[("ps", b)])
            P.op("act", lambda e, b=b, bg=bg: e.activation(out=dt_tok[:, bg, :], in_=ps[b][:, 56:64], func=AF.Copy),
                 r=[("ps", b), "spt"] + A, w=["dt"])

    import os as _os
    _stage = _os.environ.get("SSD_STAGE", "all")
    if _stage == "inproj":
        return
    XK = ["wsd", "xr", "dg"] + [("xc", c) for c in range(4)]
    P.op("dve", lambda e: e.memset(sm[:, 0:1], 0.0), r=XK, w=XK + ["arena3"])
    A3 = ["arena3"]
    dtf = dt_tok.rearrange("p b r -> p (b r)")
    P.op("dve", lambda e: e.tensor_tensor(out=dt_tok[:, :, :], in0=dt_tok[:, :, :], in1=spt[:, 40:48].unsqueeze(1).to_broadcast([128, NBLK, 8]),
                                          op=ALU.add), r=["dt", "spt"], w=["dt"])
    P.op("act", lambda e: e.activation(out=dtf, in_=dtf, func=AF.Exp), r=["dt"], w=["dt"])
    P.op("act", lambda e: e.activation(out=dtf, in_=dtf, func=AF.Ln, bias=cst[:, 0:1]), r=["dt", "cst"], w=["dt"])
    P.op("dve", lambda e: e.memset(dt_tok[0:PADN, 0, :], 0.0), r=["dt"], w=["dt"])
    P.op("act", lambda e: e.activation(out=abc[:, :], in_=spt[:, 48:56], func=AF.Exp), r=["spt"] + A3, w=["abc"])
    P.op("dve", lambda e: e.tensor_scalar(out=abc[:, :], in0=abc[:, :], scalar1=-1.0, scalar2=None, op0=ALU.mult), r=["abc"], w=["abc"])
    P.op("dve", lambda e: e.tensor_tensor(out=ablk[:, :, :], in0=dt_tok[:, :, :], in1=abc[:, :].unsqueeze(1).to_broadcast([128, NBLK, 8]),
                                          op=ALU.mult), r=["dt", "abc"] + A3, w=["ablk"])
    P.op("dve", lambda e: e.memset(TRI[:, :], 1.0), r=A3, w=["TRI"])
    P.op("pool", lambda e: e.affine_select(out=TRI[:, :], in_=TRI[:, :], pattern=[[1, 128]], compare_op=ALU.is_ge, fill=0.0,
                                           base=0, channel_multiplier=-1), r=["TRI"], w=["TRI"])
    P.op("dve", lambda e: e.memset(LSEL[:, :], 0.0), r=A3, w=["LSEL"])
    P.op("pool", lambda e: e.affine_select(out=LSEL[:, :], in_=LSEL[:, :], pattern=[[0, 128]], compare_op=ALU.not_equal, fill=1.0,
                                           base=-127, channel_multiplier=1), r=["LSEL"], w=["LSEL"])
    P.op("dve", lambda e: e.memset(SEL[0:8, :, :], 0.0), r=A3, w=["SEL"])
    P.op("pool", lambda e: e.affine_select(out=SEL[0:8, :, :], in_=SEL[0:8, :, :], pattern=[[-1, 8], [0, 128]], compare_op=ALU.not_equal,
                                           fill=1.0, base=0, channel_multiplier=1), r=["SEL"], w=["SEL"])
    for r in range(8):
        P.op("dve", lambda e, r=r: e.tensor_scalar(out=dI[:, r, :], in0=ident[:, :], scalar1=spt[:, 56 + r:57 + r], scalar2=None,
                                                   op0=ALU.mult), r=["ident", "spt"] + A3, w=["dI"])
    P.op("dve", lambda e: e.memset(H[:, :], 0.0), r=A3, w=["H"])
    P.op("dve", lambda e: e.memset(Hb[:, :], 0.0), r=A3, w=["Hb"])
    ablkf = ablk.rearrange("p b r -> p (b r)")
    acsf = acs.rearrange("p b r -> p (b r)")
    P.op("pe", lambda e: e.matmul(ps[7][:, 0:264], lhsT=TRI[:, :], rhs=ablkf, start=True, stop=True), r=["TRI", "ablk"], w=[("ps", 7)])
    P.op("act", lambda e: e.activation(out=acsf, in_=ps[7][:, 0:264], func=AF.Copy), r=[("ps", 7)] + A3, w=["acs"])
    P.op("act", lambda e: e.activation(out=Et.rearrange("p b r -> p (b r)"), in_=ps[7][:, 0:264], func=AF.Exp), r=[("ps", 7)] + A3, w=["Et"])
    P.op("pe", lambda e: e.matmul(ps[7][:, 0:264], lhsT=LSEL[:, :], rhs=acsf, start=True, stop=True), r=["LSEL", "acs"], w=[("ps", 7)])
    P.op("act", lambda e: e.activation(out=cd.rearrange("p b r -> p (b r)"), in_=ps[7][:, 0:264], func=AF.Exp), r=[("ps", 7)] + A3, w=["cd"])
    dstf = dst.rearrange("p b r -> p (b r)")
    P.op("dve", lambda e: e.tensor_tensor(out=dstf, in0=ps[7][:, 0:264], in1=acsf, op=ALU.subtract), r=[("ps", 7), "acs"] + A3, w=["dst"])
    P.op("act", lambda e: e.activation(out=dstf, in_=dstf, func=AF.Exp), r=["dst"], w=["dst"])
    P.op("dve", lambda e: e.tensor_tensor(out=w1.rearrange("p b r -> p (b r)"), in0=dtf, in1=dstf, op=ALU.mult), r=["dt", "dst"] + A3, w=["w1"])

    yci = 0
    if _stage == "pre":
        return
    _nch = int(_stage.split(":")[1]) if _stage.startswith("chunks:") else NBLK
    for c in range(_nch):
        P.op("pe", lambda e, c=c: e.matmul(ps[0][0:8, 128:256], lhsT=ablk[:, c, :], rhs=TRI[:, :], start=True, stop=True),
             r=["ablk", "TRI"], w=[("ps", 0)])
        P.op("act", lambda e: e.activation(out=acsT[0:8, 0:128], in_=ps[0][0:8, 128:256], func=AF.Copy), r=[("ps", 0)] + A3, w=["acsT"])
        P.op("act", lambda e: e.activation(out=acsT[0:8, 128:256], in_=ps[0][0:8, 128:256], func=AF.Copy, scale=-1.0), r=[("ps", 0)] + A3, w=["nacsT"])
        P.op("pe", lambda e, c=c: e.matmul(ps[0][:, 0:128], lhsT=BT[:, c * 128:(c + 1) * 128], rhs=CT[:, c * 128:(c + 1) * 128],
                                           start=True, stop=True), r=[("BT", c // 4), ("CT", c // 4)], w=[("ps", 0)])
        for half in range(2):
            bd = 1 + half

            def dm(e, half=half, bd=bd):
                for r4 in range(4):
                    r = half * 4 + r4
                    o_ = ps[bd][:, r4 * 128:(r4 + 1) * 128]
                    e.matmul(o_, lhsT=SEL[0:8, r, :], rhs=acsT[0:8, 0:128], start=True, stop=False)
                    e.matmul(o_, lhsT=acsT[0:8, 128:256], rhs=SEL[0:8, r, :], start=False, stop=False)
                    ins = e.matmul(o_, lhsT=ident[:, :], rhs=mask[:, 0, 0:128], start=False, stop=True)
                return ins

            P.op("pe", dm, r=["SEL", "acsT", "nacsT", "ident", "mask"], w=[("ps", bd)])
            P.op("act", lambda e, half=half, bd=bd: e.activation(out=dec[:, half * 512:(half + 1) * 512], in_=ps[bd][:, :], func=AF.Exp),
                 r=[("ps", bd)] + A3, w=[("dec", half)])
            P.op("dve", lambda e, half=half: e.tensor_tensor(
                out=G[:, half * 4:(half + 1) * 4, :], in0=dec[:, half * 512:(half + 1) * 512].rearrange("p (r l) -> p r l", r=4),
                in1=ps[0][:, 0:128].unsqueeze(1).to_broadcast([128, 4, 128]), op=ALU.mult),
                r=[("dec", half), ("ps", 0)] + A3, w=[("G", half)])
        xs3 = xs_tok[:, c, :].rearrange("p (r d) -> p r d", r=8)
        P.op("dve", lambda e, c=c, xs3=xs3: e.tensor_tensor(out=xdt.rearrange("p (r d) -> p r d", r=8), in0=xs3,
                                                           in1=dt_tok[:, c, :].unsqueeze(2).to_broadcast([128, 8, 64]), op=ALU.mult),
             r=[("xs", c), "dt"] + A3, w=["xdt"])
        P.op("dve", lambda e, c=c, xs3=xs3: e.tensor_tensor(out=X2.rearrange("p (r d) -> p r d", r=8), in0=xs3,
                                                           in1=w1[:, c, :].unsqueeze(2).to_broadcast([128, 8, 64]), op=ALU.mult),
             r=[("xs", c), "w1"] + A3, w=["X2"])

        def yd(e, c=c):
            for r in range(8):
                o_ = ps[3][:, r * 64:(r + 1) * 64]
                e.matmul(o_, lhsT=G[:, r, :], rhs=xdt[:, r * 64:(r + 1) * 64], start=True, stop=False)
                ins = e.matmul(o_, lhsT=dI[:, r, :], rhs=xs_tok[:, c, r * 64:(r + 1) * 64], start=False, stop=True)
            return ins

        P.op("pe", yd, r=[("G", 0), ("G", 1), "xdt", "dI", ("xs", c)], w=[("ps", 3)])
        P.op("pe", lambda e, c=c: e.matmul(ps[4][:, :], lhsT=CT[:, c * 128:(c + 1) * 128], rhs=Hb[:, :], start=True, stop=True),
             r=[("CT", c // 4), "Hb"], w=[("ps", 4)])
        P.op("pe", lambda e, c=c: e.matmul(ps[5][:, :], lhsT=B_tok[:, c, :], rhs=X2[:, :], start=True, stop=True),
             r=[("Bt", c), "X2"], w=[("ps", 5)])
        P.op("dve", lambda e, c=c: e.tensor_tensor(out=yo.rearrange("p (r d) -> p r d", r=8),
                                                   in0=ps[4][:, :].rearrange("p (r d) -> p r d", r=8),
                                                   in1=Et[:, c, :].unsqueeze(2).to_broadcast([128, 8, 64]), op=ALU.mult),
             r=[("ps", 4), "Et"] + A3, w=["yo"])
        P.op("dve", lambda e: e.tensor_tensor(out=y1[:, :], in0=ps[3][:, :], in1=yo[:, :], op=ALU.add), r=[("ps", 3), "yo"] + A3, w=["y1"])
        P.op("dve", lambda e, c=c: e.tensor_tensor(out=gy[:, :], in0=y1[:, :], in1=zs[:, c, :], op=ALU.mult), r=["y1", ("zs", c)] + A3, w=["gy"])
        P.op("act", lambda e: e.activation(out=sq[:, :], in_=gy[:, :], func=AF.Square), r=["gy"] + A3, w=["sq"])
        P.op("dve", lambda e: e.tensor_reduce(out=sm[:, 0:1], in_=sq[:, :], axis=mybir.AxisListType.X, op=ALU.add), r=["sq"] + A3, w=["sm0"])
        P.op("dve", lambda e: e.tensor_scalar(out=sm[:, 1:2], in0=sm[:, 0:1], scalar1=1.0 / 512, scalar2=RMS_EPS, op0=ALU.mult, op1=ALU.add),
             r=["sm0"], w=["sm1"])
        P.op("act", lambda e: e.activation(out=sm[:, 2:3], in_=sm[:, 1:2], func=AF.Sqrt), r=["sm1"], w=["sm2"])
        P.op("dve", lambda e: e.reciprocal(out=sm[:, 3:4], in_=sm[:, 2:3]), r=["sm2"], w=["sm3"])
        P.op("dve", lambda e: e.scalar_tensor_tensor(out=yn[:, :], in0=gy[:, :], scalar=sm[:, 3:4], in1=nwt[:, :], op0=ALU.mult, op1=ALU.mult),
             r=["gy", "sm3", "nwt"] + A3, w=["yn"])

        def trn(e):
            for j in range(4):
                ins = e.transpose(psb[6][:, j * 128:(j + 1) * 128], yn[:, j * 128:(j + 1) * 128], ident[:, :])
            return ins

        P.op("pe", trn, r=["yn", "ident"], w=[("ps", 6)])
        k = yci % 2
        yci += 1
        P.op("act", lambda e, k=k: e.activation(out=ycT[k][:, :], in_=psb[6][:, 0:512], func=AF.Copy), r=[("ps", 6)] + A3, w=[("ycT", k)])
        lo = PADN if c == 0 else 0
        P.op("sp", lambda e, c=c, k=k, lo=lo: e.dma_start(
            out=yout[512:1024, c * 128 + lo - PADN:(c + 1) * 128 - PADN].rearrange("(j p) t -> p j t", p=128),
            in_=ycT[k].rearrange("p (j l) -> p j l", j=4)[:, :, lo:128]), r=[("ycT", k)], dma=True)
        if c < NBLK - 1:
            P.op("dve", lambda e, c=c: e.tensor_tensor(out=tmpH.rearrange("p (r d) -> p r d", r=8), in0=H.rearrange("p (r d) -> p r d", r=8),
                                                       in1=cd[:, c, :].unsqueeze(2).to_broadcast([128, 8, 64]), op=ALU.mult),
                 r=["H", "cd"], w=["tmpH"])
            P.op("dve", lambda e: e.tensor_tensor(out=H[:, :], in0=tmpH[:, :], in1=ps[5][:, :], op=ALU.add), r=["tmpH", ("ps", 5)], w=["H"])
            P.op("act", lambda e: e.activation(out=Hb[:, :], in_=H[:, :], func=AF.Copy), r=["H"], w=["Hb"])


_PROGS = {}
WINDOWS = (2, 4, 8, 16)


def _prog(kind):
    if kind not in _PROGS:
        if kind == "B":
            _PROGS[kind] = build_mix_program()
        else:
            _PROGS[kind] = build_tok_program(kind)
    return _PROGS[kind]


def _lnp(gs_bs):
    a = np.stack(gs_bs, 0).astype(np.float32)
    return np.ascontiguousarray(a.reshape(6, 16, 128).transpose(2, 0, 1))


def mix_inputs(hh, w_in, b_fgate, pool_w, pool_scale, conv_w, conv_b, dt_bias, a_log, d_skip, ssd_norm_w):
    c = np.ascontiguousarray
    w_at = np.concatenate([w_in[:, 256 * hh:256 * hh + 256], w_in[:, 512 + 256 * hh:512 + 256 * hh + 256],
                           w_in[:, 1024 + 256 * hh:1024 + 256 * hh + 256], w_in[:, 1536 + 4 * hh:1536 + 4 * hh + 4]], axis=1)
    w_pl = w_in[:, 1544 + 256 * hh:1544 + 256 * hh + 256]
    w_sd = np.concatenate([w_in[:, 2056 + 512 * hh:2056 + 512 * hh + 512], w_in[:, 3080 + 512 * hh:3080 + 512 * hh + 512],
                           w_in[:, 4104 + 128 * hh:4104 + 128 * hh + 128], w_in[:, 4360 + 128 * hh:4360 + 128 * hh + 128],
                           w_in[:, 4616 + 8 * hh:4616 + 8 * hh + 8]], axis=1)
    sp = np.zeros((128, NSP), np.float32)
    rc = np.ones((2, 128, LP), np.float32)
    tok = np.arange(LP - PADN)
    for i in range(2):
        g = 2 * hh + i
        w = WINDOWS[g]
        sp[:, i * 4:i * 4 + 4] = np.array([1.0, float(w >= 4), float(w >= 8), float(w >= 16)], np.float32)[None, :]
        sp[:, 8 + i] = pool_scale[g * 128:(g + 1) * 128]
        rc[i, :, PADN:] = (1.0 / np.minimum(tok + 1, w).astype(np.float32))[None, :]
    for ch in range(6):
        if ch < 4:
            cols = slice(512 * hh + ch * 128, 512 * hh + ch * 128 + 128)
        elif ch == 4:
            cols = slice(1024 + 128 * hh, 1024 + 128 * hh + 128)
        else:
            cols = slice(1280 + 128 * hh, 1280 + 128 * hh + 128)
        sp[:, 10 + ch] = conv_b[cols]
        for j in range(4):
            sp[:, 16 + ch * 4 + j] = conv_w[j, cols]
    sp[:, 40:48] = dt_bias[None, 8 * hh:8 * hh + 8]
    sp[:, 48:56] = a_log[None, 8 * hh:8 * hh + 8]
    sp[:, 56:64] = d_skip[None, 8 * hh:8 * hh + 8]
    sp[0:4, 64] = b_fgate[4 * hh:4 * hh + 4]
    nw = np.broadcast_to(ssd_norm_w[None, 512 * hh:512 * hh + 512], (128, 512))
    return {"w_at": c(w_at), "w_pl": c(w_pl), "w_sd": c(w_sd), "spd": sp, "nwd": c(nw), "rcd": rc,
            "pwd": c(pool_w[2 * hh:2 * hh + 2])}


WO_PERM = np.concatenate([np.arange(0, 256), np.arange(512, 768), np.arange(1024, 1536),
                          np.arange(256, 512), np.arange(768, 1024), np.arange(1536, 2048)])


def _run(kind, maps):
    res = run_bass_kernel_spmd(_prog(kind), maps, core_ids=list(range(8)))
    return res.results


def kernel(x, meta, f1_gate, f1_up, f1_down, ln1_g, ln1_b, w_in, b_fgate, pool_w, pool_scale, conv_w, conv_b, dt_bias,
           a_log, d_skip, ssd_norm_w, w_out, ln2_g, ln2_b, f2_gate, f2_up, f2_down, ln3_g, ln3_b):
    f = lambda a: np.asarray(a, dtype=np.float32)
    x, meta = f(x), f(meta)
    nb = x.shape[0]
    h = np.concatenate([np.broadcast_to(meta[None], (nb, 16, D)), x], axis=1)
    hT = [np.ascontiguousarray(h[c // 2, (c % 2) * NTOK:(c % 2 + 1) * NTOK].T) for c in range(8)]
    for i in range(2):
        lnp = _lnp([f(ln1_g[i]), f(ln1_b[i]), f(ln2_g[i]), f(ln2_b[i]), f(ln3_g[i]), f(ln3_b[i])])
        r = _run("A", [{"xin": hT[c], "lnp": lnp, "wg": f(f1_gate[i]), "wu": f(f1_up[i]), "wd": f(f1_down[i])} for c in range(8)])
        h1T = [r[c]["hout"] for c in range(8)]
        mi = [mix_inputs(hh, f(w_in[i]), f(b_fgate[i]), f(pool_w[i]), f(pool_scale[i]), f(conv_w[i]), f(conv_b[i]), f(dt_bias[i]),
                         f(a_log[i]), f(d_skip[i]), f(ssd_norm_w[i])) for hh in range(2)]
        maps = []
        for c in range(8):
            b, hh = c // 2, c % 2
            m = dict(mi[hh])
            m["hin"] = np.ascontiguousarray(np.concatenate([h1T[2 * b], h1T[2 * b + 1]], axis=1))
            maps.append(m)
        r = _run("B", maps)
        yo = [r[c]["yout"] for c in range(8)]
        wo = np.ascontiguousarray(f(w_out[i])[WO_PERM])
        maps = []
        for c in range(8):
            b, hh = c // 2, c % 2
            sl = slice(hh * NTOK, (hh + 1) * NTOK)
            yin = np.ascontiguousarray(np.concatenate([yo[2 * b][:, sl], yo[2 * b + 1][:, sl]], axis=0))
            maps.append({"xin": h1T[c], "yin": yin, "lnp": lnp, "wo": wo, "wg": f(f2_gate[i]), "wu": f(f2_up[i]), "wd": f(f2_down[i])})
        r = _run("C", maps)
        hT = [r[c]["hout"] for c in range(8)]
    out = np.stack([np.concatenate([hT[2 * b].T, hT[2 * b + 1].T], axis=0)[16:] for b in range(nb)], axis=0)
    return np.ascontiguousarray(out.astype(np.float32))
```
